# Optimizing a Trainium2 kernel written in Bass

```python
import jax, jax.numpy as jnp
from jax import lax
import numpy as np

D_MODEL = 4096
BATCH = 2
SEQ = 8192
DEPTH = 2

N_MIXERS = 2
N_CONV_LAYERS = (DEPTH + N_MIXERS - 1) // N_MIXERS
N_ATTN_LAYERS = DEPTH // N_MIXERS
N_META = 16
GRID_W = 64
D_FF = 11008
N_HEADS = 32
N_KV_HEADS = 8
HEAD_DIM = D_MODEL // N_HEADS
GROUP = N_HEADS // N_KV_HEADS
ROPE_AXIS_DIM = HEAD_DIM // 2
ROPE_THETA = 10000.0
Q_BLOCK = 128
CONV_WIDTH = 3
NORM_EPS = 1e-6
MACARON_WEIGHT = 0.5

kernel_name = "hybrid_conv_axial_gqa_macaron_encoder"


def _rmsnorm(x, g):
    xf = x.astype(jnp.float32)
    y = xf * lax.rsqrt(jnp.mean(xf * xf, axis=-1, keepdims=True) + NORM_EPS)
    return (y * g.astype(jnp.float32)).astype(x.dtype)


def _swiglu(x, w_gate, w_up, w_down):
    return (jax.nn.silu(x @ w_gate) * (x @ w_up)) @ w_down


def _short_conv_mixer(x, w_in, conv_w, conv_b, w_out):
    b_gate, c_gate, h = jnp.split(x @ w_in, 3, axis=-1)
    u = c_gate * h
    n = u.shape[1]
    up = jnp.pad(u, ((0, 0), (1, 1), (0, 0)))
    conv = (up[:, 0:n] * conv_w[0] + up[:, 1:n + 1] * conv_w[1]
            + up[:, 2:n + 2] * conv_w[2] + conv_b)
    return (b_gate * conv) @ w_out


def _axial_angles(n_real):
    rows = n_real // GRID_W
    real_row = jnp.repeat(jnp.arange(rows, dtype=jnp.float32), GRID_W)
    real_col = jnp.tile(jnp.arange(GRID_W, dtype=jnp.float32), rows)
    row = jnp.concatenate([jnp.full((N_META,), -1.0, jnp.float32), real_row])
    col = jnp.concatenate([jnp.arange(N_META, dtype=jnp.float32), real_col])
    inv_freq = ROPE_THETA ** (-jnp.arange(0, ROPE_AXIS_DIM, 2, dtype=jnp.float32) / ROPE_AXIS_DIM)
    return row[:, None] * inv_freq, col[:, None] * inv_freq


def _rope_axis(t, ang):
    half = t.shape[-1] // 2
    t1, t2 = t[..., :half], t[..., half:]
    cos = jnp.cos(ang)[None, :, None, :]
    sin = jnp.sin(ang)[None, :, None, :]
    return jnp.concatenate([t1 * cos - t2 * sin, t2 * cos + t1 * sin], axis=-1)


def _qk_prep(t, gain, row_ang, col_ang):
    tf = t.astype(jnp.float32)
    tf = tf * lax.rsqrt(jnp.mean(tf * tf, axis=-1, keepdims=True) + NORM_EPS) * gain.astype(jnp.float32)
    tf = jnp.concatenate([_rope_axis(tf[..., :ROPE_AXIS_DIM], row_ang),
                          _rope_axis(tf[..., ROPE_AXIS_DIM:], col_ang)], axis=-1)
    return tf.astype(t.dtype)


def _attend_block(qb, k, v):
    s = jnp.einsum('bqkgd,bskd->bkgqs', qb, k).astype(jnp.float32) * (HEAD_DIM ** -0.5)
    p = jax.nn.softmax(s, axis=-1).astype(v.dtype)
    return jnp.einsum('bkgqs,bskd->bqkgd', p, v)


def _axial_gqa_attention(x, w_qkv, q_gain, k_gain, w_o, row_ang, col_ang):
    bsz, n, _ = x.shape
    n_real = n - N_META
    q, k, v = jnp.split(x @ w_qkv, [N_HEADS * HEAD_DIM, (N_HEADS + N_KV_HEADS) * HEAD_DIM], axis=-1)
    q = _qk_prep(q.reshape(bsz, n, N_HEADS, HEAD_DIM), q_gain, row_ang, col_ang)
    k = _qk_prep(k.reshape(bsz, n, N_KV_HEADS, HEAD_DIM), k_gain, row_ang, col_ang)
    v = v.reshape(bsz, n, N_KV_HEADS, HEAD_DIM)
    q = q.reshape(bsz, n, N_KV_HEADS, GROUP, HEAD_DIM)
    attend = lambda qb: _attend_block(qb, k, v)
    out_meta = attend(q[:, :N_META])
    q_real = q[:, N_META:].reshape(bsz, n_real // Q_BLOCK, Q_BLOCK, N_KV_HEADS, GROUP, HEAD_DIM)
    out_real = lax.map(attend, jnp.swapaxes(q_real, 0, 1))
    out_real = jnp.swapaxes(out_real, 0, 1).reshape(bsz, n_real, N_KV_HEADS, GROUP, HEAD_DIM)
    out = jnp.concatenate([out_meta, out_real], axis=1).reshape(bsz, n, N_HEADS * HEAD_DIM)
    return out @ w_o


def setup_inputs(seed: int = 0) -> dict:
    key = jax.random.key(seed)
    ks = iter(jax.random.split(key, 32))
    f32 = jnp.float32

    def w(shape, fan_in):
        return jax.random.normal(next(ks), shape, f32) * (fan_in ** -0.5)

    def gain(shape):
        return 1.0 + 0.05 * jax.random.normal(next(ks), shape, f32)

    qkv_width = (N_HEADS + 2 * N_KV_HEADS) * HEAD_DIM
    return {
        "x": jax.random.normal(next(ks), (BATCH, SEQ, D_MODEL), f32),
        "meta_tokens": jax.random.normal(next(ks), (N_META, D_MODEL), f32),
        "ffn_a_norm": gain((DEPTH, D_MODEL)),
        "ffn_a_w_gate": w((DEPTH, D_MODEL, D_FF), D_MODEL),
        "ffn_a_w_up": w((DEPTH, D_MODEL, D_FF), D_MODEL),
        "ffn_a_w_down": w((DEPTH, D_FF, D_MODEL), D_FF),
        "ffn_b_norm": gain((DEPTH, D_MODEL)),
        "ffn_b_w_gate": w((DEPTH, D_MODEL, D_FF), D_MODEL),
        "ffn_b_w_up": w((DEPTH, D_MODEL, D_FF), D_MODEL),
        "ffn_b_w_down": w((DEPTH, D_FF, D_MODEL), D_FF),
        "conv_norm": gain((N_CONV_LAYERS, D_MODEL)),
        "conv_w_in": w((N_CONV_LAYERS, D_MODEL, 3 * D_MODEL), D_MODEL),
        "conv_w": w((N_CONV_LAYERS, CONV_WIDTH, D_MODEL), CONV_WIDTH),
        "conv_b": 0.02 * jax.random.normal(next(ks), (N_CONV_LAYERS, D_MODEL), f32),
        "conv_w_out": w((N_CONV_LAYERS, D_MODEL, D_MODEL), D_MODEL),
        "attn_norm": gain((N_ATTN_LAYERS, D_MODEL)),
        "attn_w_qkv": w((N_ATTN_LAYERS, D_MODEL, qkv_width), D_MODEL),
        "attn_q_norm": gain((N_ATTN_LAYERS, HEAD_DIM)),
        "attn_k_norm": gain((N_ATTN_LAYERS, HEAD_DIM)),
        "attn_w_o": w((N_ATTN_LAYERS, N_HEADS * HEAD_DIM, D_MODEL), N_HEADS * HEAD_DIM),
        "final_norm": gain((D_MODEL,)),
    }


def reference(x, meta_tokens, ffn_a_norm, ffn_a_w_gate, ffn_a_w_up, ffn_a_w_down,
              ffn_b_norm, ffn_b_w_gate, ffn_b_w_up, ffn_b_w_down,
              conv_norm, conv_w_in, conv_w, conv_b, conv_w_out,
              attn_norm, attn_w_qkv, attn_q_norm, attn_k_norm, attn_w_o,
              final_norm):
    bsz, n_real, d = x.shape
    meta = jnp.broadcast_to(meta_tokens.astype(x.dtype)[None], (bsz, N_META, d))
    h = jnp.concatenate([meta, x], axis=1)
    row_ang, col_ang = _axial_angles(n_real)

    for i in range(DEPTH):
        h = h + MACARON_WEIGHT * _swiglu(_rmsnorm(h, ffn_a_norm[i]),
                                         ffn_a_w_gate[i], ffn_a_w_up[i], ffn_a_w_down[i])
        j = i // N_MIXERS
        if i % N_MIXERS == 0:
            h = h + _short_conv_mixer(_rmsnorm(h, conv_norm[j]),
                                      conv_w_in[j], conv_w[j], conv_b[j], conv_w_out[j])
        else:
            h = h + _axial_gqa_attention(_rmsnorm(h, attn_norm[j]), attn_w_qkv[j],
                                         attn_q_norm[j], attn_k_norm[j], attn_w_o[j],
                                         row_ang, col_ang)
        h = h + MACARON_WEIGHT * _swiglu(_rmsnorm(h, ffn_b_norm[i]),
                                         ffn_b_w_gate[i], ffn_b_w_up[i], ffn_b_w_down[i])

    h = _rmsnorm(h, final_norm)
    return h[:, N_META:]
```

```python
import numpy as np
import concourse.bass as bass
import concourse.mybir as mybir
from concourse.bass_utils import run_bass_kernel_spmd

F32 = mybir.dt.float32
BF16 = mybir.dt.bfloat16
AF = mybir.ActivationFunctionType
ALU = mybir.AluOpType


class Cfg:
    def __init__(self, D=4096, DFF=11008, NH=32, NKV=8, SEQ=8192, NMETA=16, GRID_W=64,
                 NPC=29, TMAX=512, CAST_ROWS=512, debug=False, KGMAX=None):
        self.D, self.DFF, self.NH, self.NKV, self.SEQ = D, DFF, NH, NKV, SEQ
        self.NMETA, self.GRID_W = NMETA, GRID_W
        self.HD = 128
        assert D == NH * 128 and DFF % 128 == 0
        self.KC = D // 128
        self.FC = DFF // 128
        self.N = NMETA + SEQ
        self.NQ = SEQ // 4
        self.GROUP = NH // NKV
        self.NPC = NPC
        self.TMAX = TMAX
        self.TQ = min(TMAX, self.NQ)
        self.NKMAX = max(self.KC, NPC)
        self.CAST_ROWS = CAST_ROWS
        self.debug = debug
        parts, c = [], 0
        npart = -(-self.FC // NPC)
        base, rem = divmod(self.FC, npart)
        for i in range(npart):
            n = base + (1 if i < rem else 0)
            parts.append((c, n))
            c += n
        self.parts = parts
        nown = self.NQ // self.TQ
        restn = self.N - self.NQ
        nt = -(-restn // TMAX)
        rs = [2 * ((restn // 2) // nt)] * nt
        left = restn - sum(rs)
        i = 0
        while left > 0:
            rs[i] += 2
            left -= 2
            i += 1
        sizes = [self.TQ] * nown + rs
        assert sum(sizes) == self.N and max(sizes) <= TMAX
        self.tiles1, t = [], 0
        for s in sizes:
            self.tiles1.append((t, s))
            t += s
        self.tiles2 = self.tiles1[:nown]
        self.NT = len(self.tiles1)
        self.NCH = -(-self.N // 128)
        self.VW = NKV * 128
        self.VB = min(512, self.VW)
        self.KG = min(self.KC, 8192 // self.VB)
        if KGMAX:
            self.KG = min(self.KG, KGMAX)


def _tile_proj(W, kchunks, pairs):
    K, Nc = W.shape
    Wr = W.reshape(K // 128, 128, Nc // 128, 128)
    flat = [c for pr in pairs for c in pr]
    A = Wr[np.asarray(kchunks)][:, :, np.asarray(flat), :]
    nk = len(kchunks)
    A = A.reshape(nk, 128, len(pairs), 2, 128).transpose(2, 1, 3, 0, 4)
    return np.ascontiguousarray(A).reshape(len(pairs) * 128, 2 * nk * 128)


def _tile_v(Wv, cfg):
    out = []
    for blk in range(cfg.VW // cfg.VB):
        for half in range(cfg.KC // cfg.KG):
            sub = Wv[half * cfg.KG * 128:(half + 1) * cfg.KG * 128, blk * cfg.VB:(blk + 1) * cfg.VB]
            out.append(sub.reshape(cfg.KG, 128, cfg.VB).transpose(1, 0, 2).reshape(128, cfg.KG * cfg.VB))
    return np.ascontiguousarray(np.concatenate(out, axis=0))


def _gain_cols(g, kc):
    return np.ascontiguousarray(g.reshape(kc, 128).T)


def weight_streams(cfg, inp):
    KC, FC = cfg.KC, cfg.FC
    ws = {}

    def ffn(tag, wg, wu, wd):
        cat = np.concatenate([wg, wu], axis=1)
        ws[f"gu_{tag}"] = _tile_proj(cat, list(range(KC)), [(c, FC + c) for c in range(FC)])
        del cat
        for pi, (c0, n) in enumerate(cfg.parts):
            ws[f"dn_{tag}_{pi}"] = _tile_proj(wd, list(range(c0, c0 + n)),
                                               [(2 * i, 2 * i + 1) for i in range(KC // 2)])

    ffn("a0", inp["ffn_a_w_gate"][0], inp["ffn_a_w_up"][0], inp["ffn_a_w_down"][0])
    win = inp["conv_w_in"][0]
    pairs = [(KC + f, 2 * KC + f) for f in range(KC)] + [(2 * i, 2 * i + 1) for i in range(KC // 2)]
    ws["cin"] = _tile_proj(win, list(range(KC)), pairs)
    ws["cout"] = _tile_proj(inp["conv_w_out"][0], list(range(KC)), [(2 * i, 2 * i + 1) for i in range(KC // 2)])
    ffn("b0", inp["ffn_b_w_gate"][0], inp["ffn_b_w_up"][0], inp["ffn_b_w_down"][0])
    ffn("a1", inp["ffn_a_w_gate"][1], inp["ffn_a_w_up"][1], inp["ffn_a_w_down"][1])
    wqkv = inp["attn_w_qkv"][0]
    NH, NKV = cfg.NH, cfg.NKV
    kch = [NH + g for g in range(NKV)]
    if len(kch) % 2:
        kch = kch + [kch[-1]]
    ws["wk"] = _tile_proj(wqkv, list(range(KC)), [(kch[2 * i], kch[2 * i + 1]) for i in range(len(kch) // 2)])
    ws["wv"] = _tile_v(wqkv[:, (NH + NKV) * 128:], cfg)
    ws["wq"] = _tile_proj(wqkv, list(range(KC)), [(2 * i, 2 * i + 1) for i in range(NH // 2)])
    ws["wo"] = _tile_proj(inp["attn_w_o"][0], list(range(KC)), [(2 * i, 2 * i + 1) for i in range(KC // 2)])
    ffn("b1", inp["ffn_b_w_gate"][1], inp["ffn_b_w_up"][1], inp["ffn_b_w_down"][1])
    return ws


def stream_shapes(cfg):
    KC, FC = cfg.KC, cfg.FC
    sh = {}

    def ffn(tag):
        sh[f"gu_{tag}"] = (FC * 128, 2 * KC * 128)
        for pi, (c0, n) in enumerate(cfg.parts):
            sh[f"dn_{tag}_{pi}"] = ((KC // 2) * 128, 2 * n * 128)

    ffn("a0")
    sh["cin"] = ((KC + KC // 2) * 128, 2 * KC * 128)
    sh["cout"] = ((KC // 2) * 128, 2 * KC * 128)
    ffn("b0")
    ffn("a1")
    sh["wk"] = (((cfg.NKV + 1) // 2) * 128, 2 * KC * 128)
    sh["wv"] = ((cfg.VW // cfg.VB) * (KC // cfg.KG) * 128, cfg.KG * cfg.VB)
    sh["wq"] = ((cfg.NH // 2) * 128, 2 * KC * 128)
    sh["wo"] = ((KC // 2) * 128, 2 * KC * 128)
    ffn("b1")
    return sh


def block_tiles(cfg, hT, tiles):
    out = np.zeros((len(tiles), 128, cfg.KC * cfg.TMAX), np.float32)
    for i, (t0, T) in enumerate(tiles):
        out[i, :, 0:cfg.KC * T] = hT[:, t0:t0 + T].reshape(cfg.KC, 128, T).transpose(1, 0, 2).reshape(128, cfg.KC * T)
    return out


def unblock_tiles(cfg, blk, tiles, ntok):
    hT = np.empty((cfg.D, ntok), np.float32)
    for i, (t0, T) in enumerate(tiles):
        hT[:, t0:t0 + T] = blk[i, :, 0:cfg.KC * T].reshape(128, cfg.KC, T).transpose(1, 0, 2).reshape(cfg.D, T)
    return hT


def token_order(cfg, qi):
    NM, NQ, SEQ = cfg.NMETA, cfg.NQ, cfg.SEQ
    real = np.arange(SEQ) + NM
    return np.concatenate([real[qi * NQ:], np.arange(NM), real[:qi * NQ]])


def rope_tables(cfg, order):
    NM, GW = cfg.NMETA, cfg.GRID_W
    n = np.arange(cfg.N)
    row = np.where(n < NM, -1.0, (n - NM) // GW).astype(np.float64)
    col = np.where(n < NM, n, (n - NM) % GW).astype(np.float64)
    inv = 10000.0 ** (-np.arange(0, 64, 2, dtype=np.float64) / 64.0)
    inv = inv.astype(np.float32).astype(np.float64)
    ra = (row[:, None].astype(np.float32) * inv[None].astype(np.float32)).astype(np.float64)
    ca = (col[:, None].astype(np.float32) * inv[None].astype(np.float32)).astype(np.float64)
    cos = np.concatenate([np.cos(ra), np.cos(ra), np.cos(ca), np.cos(ca)], axis=1)
    sin = np.concatenate([-np.sin(ra), np.sin(ra), -np.sin(ca), np.sin(ca)], axis=1)
    cs = np.stack([cos[order].T, sin[order].T], axis=1)
    return np.ascontiguousarray(cs.astype(np.float32))


def perm_matrix():
    P = np.zeros((128, 128), np.float32)
    for m in range(128):
        blk, off = divmod(m, 32)
        k = m + 32 if blk % 2 == 0 else m - 32
        P[k, m] = 1.0
    return P


def conv_masks(cfg, order):
    m = np.ones((2, cfg.N), np.float32)
    m[0, order == 0] = 0.0
    m[1, order == cfg.N - 1] = 0.0
    return np.ascontiguousarray(np.broadcast_to(m[None], (128, 2, cfg.N)))


def small_consts(cfg, inp):
    KC = cfg.KC
    cols, off, parts = {}, 0, []

    def add(name, arr):
        nonlocal off
        cols[name] = (off, arr.shape[1])
        parts.append(arr.astype(np.float32))
        off += arr.shape[1]

    add("g_a0", _gain_cols(inp["ffn_a_norm"][0], KC))
    add("g_b0", _gain_cols(inp["ffn_b_norm"][0], KC))
    add("g_a1", _gain_cols(inp["ffn_a_norm"][1], KC))
    add("g_b1", _gain_cols(inp["ffn_b_norm"][1], KC))
    add("g_conv", _gain_cols(inp["conv_norm"][0], KC))
    add("g_attn", _gain_cols(inp["attn_norm"][0], KC))
    add("g_fin", _gain_cols(inp["final_norm"], KC))
    add("cw0", _gain_cols(inp["conv_w"][0, 0], KC))
    add("cw1", _gain_cols(inp["conv_w"][0, 1], KC))
    add("cw2", _gain_cols(inp["conv_w"][0, 2], KC))
    add("cb", _gain_cols(inp["conv_b"][0], KC))
    add("gq", inp["attn_q_norm"][0].reshape(128, 1))
    add("gk", inp["attn_k_norm"][0].reshape(128, 1))
    add("eps", np.full((128, 1), 1e-6, np.float32))
    add("ones", np.ones((128, 128), np.float32))
    add("perm", perm_matrix())
    return np.ascontiguousarray(np.concatenate(parts, axis=1)), cols


class Obj:
    __slots__ = ("name", "w", "r", "multi", "semk", "cnt", "excl")

    def __init__(self, name, multi=False, semk=None, excl=False):
        self.name, self.w, self.r, self.multi, self.semk, self.cnt = name, {}, {}, multi, semk, 0
        self.excl = excl


class Prog:
    ENG = ("pe", "act", "dve", "pool", "sp")

    def __init__(self, nc):
        self.nc = nc
        self.ops = {e: [] for e in self.ENG}
        self.cnt = {e: 0 for e in self.ENG}
        self.waited = {e: {} for e in self.ENG}
        self.semh = {}
        self.nins = 0

    def add_sem(self, key, handle):
        self.semh[key] = handle

    def _need(self, eng, deps):
        for k, v in deps.items():
            if k == eng and eng == "pe":
                continue
            if self.waited[eng].get(k, 0) >= v:
                continue
            self.waited[eng][k] = v
            h = self.semh[k]
            self.ops[eng].append(lambda e, h=h, v=v: e.wait_ge(h, v))

    @staticmethod
    def _deps(reads, writes):
        deps = {}
        for o in reads:
            for k, v in o.w.items():
                if deps.get(k, 0) < v:
                    deps[k] = v
            if o.excl:
                for k, v in o.r.items():
                    if deps.get(k, 0) < v:
                        deps[k] = v
        for o in writes:
            if not o.multi:
                for k, v in o.w.items():
                    if deps.get(k, 0) < v:
                        deps[k] = v
            for k, v in o.r.items():
                if deps.get(k, 0) < v:
                    deps[k] = v
        return deps

    @staticmethod
    def _mark(k, v, reads, writes):
        for o in reads:
            if o.r.get(k, 0) < v:
                o.r[k] = v
        for o in writes:
            if o.multi:
                if o.w.get(k, 0) < v:
                    o.w[k] = v
            else:
                o.w = {k: v}
                o.r = {}

    def op(self, eng, meth, kw, reads=(), writes=()):
        self._need(eng, self._deps(reads, writes))
        self.cnt[eng] += 1
        v = self.cnt[eng]
        h = self.semh[eng]
        self.ops[eng].append(lambda e, meth=meth, kw=kw, h=h: getattr(e, meth)(**kw).then_inc(h, 1))
        self._mark(eng, v, reads, writes)
        self.nins += 1

    def burst(self, mms, reads, banks):
        self._need("pe", self._deps(reads, banks))
        self.cnt["pe"] += 1
        v = self.cnt["pe"]
        h = self.semh["pe"]
        n = len(mms)
        for i, (o, l, r, st, sp) in enumerate(mms):
            if i == n - 1:
                self.ops["pe"].append(lambda e, o=o, l=l, r=r, st=st, sp=sp, h=h:
                                      e.matmul(o, l, r, start=st, stop=sp).then_inc(h, 1))
            else:
                self.ops["pe"].append(lambda e, o=o, l=l, r=r, st=st, sp=sp:
                                      e.matmul(o, l, r, start=st, stop=sp))
        self._mark("pe", v, reads, banks)
        self.nins += n

    def dma(self, q, out, in_, reads, writes, semobj, **kw):
        self._need(q, self._deps(reads, writes))
        semobj.cnt += 16
        v, k = semobj.cnt, semobj.semk
        h = self.semh[k]
        self.ops[q].append(lambda e, out=out, in_=in_, h=h, kw=kw: e.dma_start(out=out, in_=in_, **kw).then_inc(h, 16))
        self._mark(k, v, reads, writes)
        self.nins += 1

    def wait_obj(self, eng, objs):
        deps = {}
        for o in objs:
            for d in (o.w, o.r):
                for k, v in d.items():
                    if deps.get(k, 0) < v:
                        deps[k] = v
        for k, v in deps.items():
            if self.waited[eng].get(k, 0) >= v:
                continue
            self.waited[eng][k] = v
            h = self.semh[k]
            self.ops[eng].append(lambda e, h=h, v=v: e.wait_ge(h, v))


class Builder:
    def __init__(self, cfg, stack):
        self.cfg = cfg
        self.nc = nc = bass.Bass("TRN2", target_bir_lowering=False)
        self.P = Prog(nc)
        self.stack = stack
        c = cfg
        for e in Prog.ENG:
            self.sem(e)
        self.sh = stream_shapes(c)
        self.wf, self.wb = {}, {}
        for name, (r, w) in self.sh.items():
            self.wf[name] = nc.dram_tensor("wf_" + name, [r, w], F32, kind="ExternalInput").ap()
            self.wb[name] = nc.dram_tensor("wb_" + name, [r, w], BF16, kind="Internal").ap()
        BW = c.KC * c.TMAX
        self.h0T = nc.dram_tensor("h0T", [c.NT, 128, BW], F32, kind="ExternalInput").ap()
        self.hT = nc.dram_tensor("hT", [c.NT, 128, BW], F32, kind="Internal").ap()
        self.outT = nc.dram_tensor("outT", [len(c.tiles2), 128, BW], F32, kind="ExternalOutput").ap()
        self.cs_d = nc.dram_tensor("rope_cs", [128, 2, c.N], F32, kind="ExternalInput").ap()
        self.mk_d = nc.dram_tensor("conv_mask", [128, 2, c.N], F32, kind="ExternalInput").ap()
        self.uT = nc.dram_tensor("uT", [c.D, c.N + 2], F32, kind="Internal").ap()
        self.bgT = nc.dram_tensor("bgT", [c.D, c.N], F32, kind="Internal").ap()
        self.kT = nc.dram_tensor("kT", [c.NKV * 128, c.N], BF16, kind="Internal").ap()
        self.vv = nc.dram_tensor("vv", [c.N, c.VW], BF16, kind="Internal").ap()
        self.qT = nc.dram_tensor("qT", [c.NH * 128, c.NQ], BF16, kind="Internal").ap()
        self.o_h0 = Obj("h0T")
        self.o_hT = Obj("hT", multi=True)
        self.o_uT = Obj("uT", multi=True)
        self.o_bgT = Obj("bgT", multi=True)
        self.o_kT = Obj("kT", multi=True)
        self.o_vv = Obj("vv", multi=True)
        self.o_qT = Obj("qT", multi=True)
        self.o_out = Obj("outT", multi=True)
        self.o_din = Obj("dram_in")
        self.dbg = []

    def sem(self, key):
        self.P.add_sem(key, self.stack.enter_context(self.nc.semaphore(key)))

    def dobj(self, name, multi=False):
        self.sem("s_" + name)
        return Obj(name, multi=multi, semk="s_" + name)

    def alloc(self, ncst_cols):
        c, nc = self.cfg, self.nc
        ent = self.stack.enter_context
        dobj = self.dobj
        T = c.TMAX
        self.cst_d = nc.dram_tensor("consts", [128, ncst_cols], F32, kind="ExternalInput").ap()
        self.XIN = ent(nc.sbuf_tensor("XIN", [128, c.KC * T], F32))
        self.XN = ent(nc.sbuf_tensor("XN", [128, c.KC, T], BF16))
        self.HID = ent(nc.sbuf_tensor("HID", [128, c.NPC, T], BF16))
        self.o_XIN, self.o_XN, self.o_HID = dobj("XIN", multi=True), Obj("XN", multi=True), Obj("HID", multi=True)
        self.NSLOT = 3
        self.WS = [ent(nc.sbuf_tensor(f"WS{i}", [128, 2 * c.NKMAX * 128], BF16)) for i in range(self.NSLOT)]
        self.o_WS = [dobj(f"WS{i}") for i in range(self.NSLOT)]
        self.wi = 0
        self.CST = ent(nc.sbuf_tensor("CST", [128, ncst_cols], F32))
        self.o_CST = dobj("CST")
        self.ONEB = ent(nc.sbuf_tensor("ONEB", [128, 128], BF16))
        self.o_ONEB = Obj("ONEB")
        self.NTMP = 8
        self.TMP = [ent(nc.sbuf_tensor(f"TMP{i}", [128, T], F32)) for i in range(self.NTMP)]
        self.o_TMP = [dobj(f"TMP{i}") for i in range(self.NTMP)]
        self.ti = 0
        self.NSTG = 4
        self.STG = [ent(nc.sbuf_tensor(f"STG{i}", [128, T], BF16)) for i in range(self.NSTG)]
        self.o_STG = [dobj(f"STG{i}") for i in range(self.NSTG)]
        self.si = 0
        self.CS = ent(nc.sbuf_tensor("CS", [128, 2, T], F32))
        self.o_CS = dobj("CS")
        self.QH = [ent(nc.sbuf_tensor(f"QH{i}", [128, T], BF16)) for i in range(2)]
        self.o_QH = [dobj(f"QH{i}") for i in range(2)]
        self.PSB = [ent(nc.psum_tensor(f"PS{i}", [128, 512], F32)) for i in range(8)]
        self.o_PS = [Obj(f"PS{i}", excl=True) for i in range(8)]
        self.bi = 0
        self.cast_objs = {}

    def xv(self, T):
        return self.XIN[:, 0:self.cfg.KC * T].rearrange("p (k t) -> p k t", t=T)

    def tmp(self):
        i = self.ti
        self.ti = (i + 1) % self.NTMP
        return self.TMP[i], self.o_TMP[i]

    def stg(self):
        i = self.si
        self.si = (i + 1) % self.NSTG
        return self.STG[i], self.o_STG[i]

    def bank(self):
        i = self.bi
        self.bi = (i + 1) % 8
        return self.PSB[i], self.o_PS[i]

    def cst(self, name, j=0, n=1):
        o, w = self.cols[name]
        return self.CST[:, o + j:o + j + n]

    def emit_casts(self):
        c, P = self.cfg, self.P
        for name, (rows, width) in self.sh.items():
            ngrp = 4 if name.startswith("gu_") else 1
            gsz = -(-rows // ngrp)
            gsz = -(-gsz // 128) * 128
            objs = []
            for g0 in range(0, rows, gsz):
                g1 = min(rows, g0 + gsz)
                key = f"c_{name}_{g0}"
                self.sem(key)
                o = Obj(key, multi=True, semk=key)
                for r0 in range(g0, g1, c.CAST_ROWS):
                    r1 = min(g1, r0 + c.CAST_ROWS)
                    P.dma("pool", self.wb[name][r0:r1, :], self.wf[name][r0:r1, :], [], [o], o,
                          max_dma_last_dim=4096)
                objs.append((g0, g1, o))
            self.cast_objs[name] = objs

    def wload(self, name, t, width):
        i = self.wi
        self.wi = (i + 1) % self.NSLOT
        r0 = t * 128
        co = [o for (a, b, o) in self.cast_objs[name] if a <= r0 < b]
        assert len(co) == 1
        self.P.dma("sp", self.WS[i][:, 0:width], self.wb[name][r0:r0 + 128, :], co, [self.o_WS[i]], self.o_WS[i])
        return self.WS[i], self.o_WS[i]

    def emit_rstd(self, T):
        c, P = self.cfg, self.P
        X = self.xv(T)
        ps, ops_ = self.bank()
        for kc in range(c.KC):
            sq, osq = self.stg()
            P.op("act", "activation", dict(out=sq[:, 0:T], in_=X[:, kc, :], func=AF.Square), [self.o_XIN], [osq])
            P.burst([(ps[:, 0:T], self.ONEB[:, :], sq[:, 0:T], kc == 0, kc == c.KC - 1)], [osq, self.o_ONEB], [ops_])
        rs, ors = self.tmp()
        P.op("act", "activation", dict(out=rs[:, 0:T], in_=ps[:, 0:T], func=AF.Sqrt, scale=1.0 / c.D, bias=self.cst("eps")),
             [ops_, self.o_CST], [ors])
        P.op("dve", "reciprocal", dict(out=rs[:, 0:T], in_=rs[:, 0:T]), [ors], [ors])
        return rs, ors

    def emit_norm(self, T, gname):
        c, P = self.cfg, self.P
        X = self.xv(T)
        rs, ors = self.emit_rstd(T)
        for kc in range(c.KC):
            P.op("dve", "scalar_tensor_tensor",
                 dict(out=self.XN[:, kc, 0:T], in0=X[:, kc, :], scalar=self.cst(gname, kc), in1=rs[:, 0:T],
                      op0=ALU.mult, op1=ALU.mult), [self.o_XIN, ors, self.o_CST], [self.o_XN])

    def load_x(self, src, osrc, ti, T):
        n = self.cfg.KC * T
        self.P.dma("act", self.XIN[:, 0:n], src[ti, :, 0:n], [osrc], [self.o_XIN], self.o_XIN,
                   max_dma_last_dim=32768)

    def store_x(self, dst, odst, ti, T):
        n = self.cfg.KC * T
        self.P.dma("act", dst[ti, :, 0:n], self.XIN[:, 0:n], [self.o_XIN], [odst], self.o_XIN,
                   max_dma_last_dim=32768)

    def proj_pair(self, stream, t, nk, rhs, rhs_objs, T):
        ws, ows = self.wload(stream, t, 2 * nk * 128)
        wv = ws[:, 0:2 * nk * 128].rearrange("p (s k n) -> p s k n", s=2, k=nk)
        outs = []
        for s in range(2):
            ps, ops_ = self.bank()
            mms = [(ps[:, 0:T], wv[:, s, k, :], rhs[:, k, 0:T], k == 0, k == nk - 1) for k in range(nk)]
            self.P.burst(mms, [ows] + rhs_objs, [ops_])
            outs.append((ps, ops_))
        return outs

    def resid_pairs(self, stream, nk, rhs, rhs_objs, T, scale):
        c, P = self.cfg, self.P
        X = self.xv(T)
        for dp in range(c.KC // 2):
            outs = self.proj_pair(stream, dp, nk, rhs, rhs_objs, T)
            for s, (ps, ops_) in enumerate(outs):
                dc = 2 * dp + s
                P.op("dve", "scalar_tensor_tensor",
                     dict(out=X[:, dc, :], in0=ps[:, 0:T], scalar=scale, in1=X[:, dc, :],
                          op0=ALU.mult, op1=ALU.add), [ops_, self.o_XIN], [self.o_XIN])

    def ffn_tile(self, tag, gname, T):
        c, P = self.cfg, self.P
        XN, HID = self.XN, self.HID
        self.emit_norm(T, gname)
        for pi, (c0, n) in enumerate(c.parts):
            for j in range(n):
                (pg, opg), (pu, opu) = self.proj_pair(f"gu_{tag}", c0 + j, c.KC, XN, [self.o_XN], T)
                sg, osg = self.tmp()
                P.op("act", "activation", dict(out=sg[:, 0:T], in_=pg[:, 0:T], func=AF.Silu), [opg], [osg])
                P.op("dve", "tensor_tensor", dict(out=HID[:, j, 0:T], in0=sg[:, 0:T], in1=pu[:, 0:T], op=ALU.mult),
                     [osg, opu], [self.o_HID])
            self.resid_pairs(f"dn_{tag}_{pi}", n, HID, [self.o_HID], T, 0.5)

    def conv_in_tile(self, t0, T, after_norm=None):
        c, P = self.cfg, self.P
        XN = self.XN
        KC, N = c.KC, c.N
        self.emit_norm(T, "g_conv")
        if after_norm:
            after_norm()
        for f in range(KC):
            (pc, opc), (ph, oph) = self.proj_pair("cin", f, KC, XN, [self.o_XN], T)
            cg, ocg = self.tmp()
            P.op("act", "activation", dict(out=cg[:, 0:T], in_=pc[:, 0:T], func=AF.Copy), [opc], [ocg])
            u, ou = self.tmp()
            P.op("dve", "tensor_tensor", dict(out=u[:, 0:T], in0=cg[:, 0:T], in1=ph[:, 0:T], op=ALU.mult),
                 [ocg, oph], [ou])
            rows = slice(f * 128, (f + 1) * 128)
            P.dma("act", self.uT[rows, 1 + t0:1 + t0 + T], u[:, 0:T], [ou], [self.o_uT], ou)
            if t0 == 0:
                P.dma("act", self.uT[rows, N + 1:N + 2], u[:, 0:1], [ou], [self.o_uT], ou,
                      allow_slow_non_contiguous=True)
            if t0 + T == N:
                P.dma("act", self.uT[rows, 0:1], u[:, T - 1:T], [ou], [self.o_uT], ou,
                      allow_slow_non_contiguous=True)
        for i in range(KC // 2):
            outs = self.proj_pair("cin", KC + i, KC, XN, [self.o_XN], T)
            for s, (ps, ops_) in enumerate(outs):
                f = 2 * i + s
                b, ob = self.tmp()
                P.op("act", "activation", dict(out=b[:, 0:T], in_=ps[:, 0:T], func=AF.Copy), [ops_], [ob])
                P.dma("act", self.bgT[f * 128:(f + 1) * 128, t0:t0 + T], b[:, 0:T], [ob], [self.o_bgT], ob)

    def conv_out_tile(self, t0, T):
        c, P = self.cfg, self.P
        XN, CS = self.XN, self.CS
        P.dma("act", CS[:, :, 0:T], self.mk_d[:, :, t0:t0 + T], [self.o_din], [self.o_CS], self.o_CS)
        for f in range(c.KC):
            rows = slice(f * 128, (f + 1) * 128)
            ul, oul = self.tmp()
            P.dma("act", ul[:, 0:T], self.uT[rows, t0:t0 + T], [self.o_uT], [oul], oul)
            ur, our = self.tmp()
            P.dma("act", ur[:, 0:T], self.uT[rows, t0 + 2:t0 + 2 + T], [self.o_uT], [our], our)
            uc, ouc = self.tmp()
            P.dma("act", uc[:, 0:T], self.uT[rows, t0 + 1:t0 + 1 + T], [self.o_uT], [ouc], ouc)
            bg, obg = self.tmp()
            P.dma("act", bg[:, 0:T], self.bgT[rows, t0:t0 + T], [self.o_bgT], [obg], obg)
            w0, w1, w2, cb = self.cst("cw0", f), self.cst("cw1", f), self.cst("cw2", f), self.cst("cb", f)
            P.op("dve", "scalar_tensor_tensor",
                 dict(out=ul[:, 0:T], in0=ul[:, 0:T], scalar=w0, in1=CS[:, 0, 0:T], op0=ALU.mult, op1=ALU.mult),
                 [oul, self.o_CS, self.o_CST], [oul])
            P.op("dve", "scalar_tensor_tensor",
                 dict(out=ur[:, 0:T], in0=ur[:, 0:T], scalar=w2, in1=CS[:, 1, 0:T], op0=ALU.mult, op1=ALU.mult),
                 [our, self.o_CS, self.o_CST], [our])
            P.op("dve", "scalar_tensor_tensor",
                 dict(out=uc[:, 0:T], in0=uc[:, 0:T], scalar=w1, in1=ul[:, 0:T], op0=ALU.mult, op1=ALU.add),
                 [ouc, oul, self.o_CST], [ouc])
            P.op("dve", "tensor_tensor", dict(out=uc[:, 0:T], in0=uc[:, 0:T], in1=ur[:, 0:T], op=ALU.add),
                 [ouc, our], [ouc])
            P.op("dve", "scalar_tensor_tensor",
                 dict(out=XN[:, f, 0:T], in0=uc[:, 0:T], scalar=cb, in1=bg[:, 0:T], op0=ALU.add, op1=ALU.mult),
                 [ouc, obg, self.o_CST], [self.o_XN])
        self.resid_pairs("cout", c.KC, XN, [self.o_XN], T, 1.0)

    def qk_prep(self, ps, ops_, T, gname):
        P, CS = self.P, self.CS
        ones, perm = self.cst("ones", 0, 128), self.cst("perm", 0, 128)
        sq, osq = self.tmp()
        P.op("act", "activation", dict(out=sq[:, 0:T], in_=ps[:, 0:T], func=AF.Square), [ops_], [osq])
        qg, oqg = self.tmp()
        P.op("dve", "tensor_scalar", dict(out=qg[:, 0:T], in0=ps[:, 0:T], scalar1=self.cst(gname), scalar2=None,
                                          op0=ALU.mult), [ops_, self.o_CST], [oqg])
        pss, opss = self.bank()
        P.burst([(pss[:, 0:T], ones, sq[:, 0:T], True, True)], [osq, self.o_CST], [opss])
        psr, opsr = self.bank()
        P.burst([(psr[:, 0:T], perm, qg[:, 0:T], True, True)], [oqg, self.o_CST], [opsr])
        rs, ors = self.tmp()
        P.op("act", "activation", dict(out=rs[:, 0:T], in_=pss[:, 0:T], func=AF.Sqrt, scale=1.0 / 128.0,
                                       bias=self.cst("eps")), [opss, self.o_CST], [ors])
        P.op("dve", "reciprocal", dict(out=rs[:, 0:T], in_=rs[:, 0:T]), [ors], [ors])
        t2, ot2 = self.tmp()
        P.op("dve", "tensor_tensor", dict(out=t2[:, 0:T], in0=psr[:, 0:T], in1=CS[:, 1, 0:T], op=ALU.mult),
             [opsr, self.o_CS], [ot2])
        P.op("dve", "tensor_tensor", dict(out=qg[:, 0:T], in0=qg[:, 0:T], in1=CS[:, 0, 0:T], op=ALU.mult),
             [oqg, self.o_CS], [oqg])
        P.op("dve", "tensor_tensor", dict(out=qg[:, 0:T], in0=qg[:, 0:T], in1=t2[:, 0:T], op=ALU.add),
             [oqg, ot2], [oqg])
        st, ost = self.stg()
        P.op("dve", "tensor_tensor", dict(out=st[:, 0:T], in0=qg[:, 0:T], in1=rs[:, 0:T], op=ALU.mult),
             [oqg, ors], [ost])
        return st, ost

    def kv_tile(self, t0, T, after_norm=None):
        c, P = self.cfg, self.P
        XN = self.XN
        self.emit_norm(T, "g_attn")
        if after_norm:
            after_norm()
        P.dma("act", self.CS[:, :, 0:T], self.cs_d[:, :, t0:t0 + T], [self.o_din], [self.o_CS], self.o_CS)
        for i in range((c.NKV + 1) // 2):
            outs = self.proj_pair("wk", i, c.KC, XN, [self.o_XN], T)
            for s, (ps, ops_) in enumerate(outs):
                g = 2 * i + s
                if g >= c.NKV:
                    continue
                st, ost = self.qk_prep(ps, ops_, T, "gk")
                P.dma("act", self.kT[g * 128:(g + 1) * 128, t0:t0 + T], st[:, 0:T], [ost], [self.o_kT], ost)
        subs = [(r, min(128, T - r)) for r in range(0, T, 128)]
        nhalf = c.KC // c.KG
        for blk in range(c.VW // c.VB):
            banks = [self.bank() for _ in subs]
            for half in range(nhalf):
                ws, ows = self.wload("wv", blk * nhalf + half, c.KG * c.VB)
                wv = ws[:, 0:c.KG * c.VB].rearrange("p (k n) -> p k n", k=c.KG)
                for (r, R), (ps, ops_) in zip(subs, banks):
                    mms = [(ps[0:R, 0:c.VB], XN[:, half * c.KG + k, r:r + R], wv[:, k, :],
                            half == 0 and k == 0, half == nhalf - 1 and k == c.KG - 1) for k in range(c.KG)]
                    P.burst(mms, [ows, self.o_XN], [ops_])
            for (r, R), (ps, ops_) in zip(subs, banks):
                st, ost = self.stg()
                P.op("act", "activation", dict(out=st[0:R, 0:c.VB], in_=ps[0:R, 0:c.VB], func=AF.Copy),
                     [ops_], [ost])
                P.dma("act", self.vv[t0 + r:t0 + r + R, blk * c.VB:(blk + 1) * c.VB], st[0:R, 0:c.VB],
                      [ost], [self.o_vv], ost)

    def attn_tile(self, t0, T):
        c, P = self.cfg, self.P
        XN = self.XN
        N, NCH = c.N, c.NCH
        scale = 128.0 ** -0.5
        xflat = self.XIN.bitcast(BF16)
        assert N + NCH * 128 <= 2 * c.KC * c.TMAX
        KT = xflat[:, 0:N]
        VG = xflat[:, N:N + NCH * 128].rearrange("p (c d) -> p c d", d=128)
        full = N // 128
        rem = N - full * 128
        self.emit_norm(T, "g_attn")
        P.dma("act", self.CS[:, :, 0:T], self.cs_d[:, :, t0:t0 + T], [self.o_din], [self.o_CS], self.o_CS)
        for i in range(c.NH // 2):
            outs = self.proj_pair("wq", i, c.KC, XN, [self.o_XN], T)
            for s, (ps, ops_) in enumerate(outs):
                h = 2 * i + s
                st, ost = self.qk_prep(ps, ops_, T, "gq")
                P.dma("act", self.qT[h * 128:(h + 1) * 128, t0:t0 + T], st[:, 0:T], [ost], [self.o_qT], ost)
        hcount = 0
        VSPL = 16
        LOOK = 2
        for g in range(c.NKV):
            P.dma("act", KT, self.kT[g * 128:(g + 1) * 128, :], [self.o_kT], [self.o_XIN], self.o_XIN)
            for c0 in range(0, full, VSPL):
                c1 = min(full, c0 + VSPL)
                P.dma("act", VG[:, c0:c1, :],
                      self.vv[c0 * 128:c1 * 128, g * 128:(g + 1) * 128].rearrange("(c p) d -> p c d", p=128),
                      [self.o_vv], [self.o_XIN], self.o_XIN)
            if rem:
                P.dma("act", VG[0:rem, full, :], self.vv[full * 128:N, g * 128:(g + 1) * 128],
                      [self.o_vv], [self.o_XIN], self.o_XIN)
            for hq in range(c.GROUP):
                h = g * c.GROUP + hq
                par = hcount % 2
                hcount += 1
                qh, oqh = self.QH[par], self.o_QH[par]
                P.dma("act", qh[:, 0:T], self.qT[h * 128:(h + 1) * 128, t0:t0 + T], [self.o_qT], [oqh], oqh)
                po, opo = self.PSB[4 + 2 * par], self.o_PS[4 + 2 * par]
                pl, opl = self.PSB[5 + 2 * par], self.o_PS[5 + 2 * par]

                def s_mm(ch):
                    kn = min(128, N - ch * 128)
                    ps, ops_ = self.PSB[ch % 4], self.o_PS[ch % 4]
                    P.burst([(ps[0:kn, 0:T], KT[:, ch * 128:ch * 128 + kn], qh[:, 0:T], True, True)],
                            [self.o_XIN, oqh], [ops_])

                for ch in range(min(LOOK, NCH)):
                    s_mm(ch)
                for ch in range(NCH):
                    kn = min(128, N - ch * 128)
                    ps, ops_ = self.PSB[ch % 4], self.o_PS[ch % 4]
                    pt, opt = self.stg()
                    P.op("act", "activation", dict(out=pt[0:kn, 0:T], in_=ps[0:kn, 0:T], func=AF.Exp, scale=scale),
                         [ops_], [opt])
                    if ch + LOOK < NCH:
                        s_mm(ch + LOOK)
                    P.burst([(po[:, 0:T], VG[0:kn, ch, :], pt[0:kn, 0:T], ch == 0, ch == NCH - 1),
                             (pl[:, 0:T], self.ONEB[0:kn, :], pt[0:kn, 0:T], ch == 0, ch == NCH - 1)],
                            [self.o_XIN, opt, self.o_ONEB], [opo, opl])
                rl, orl = self.tmp()
                P.op("dve", "reciprocal", dict(out=rl[:, 0:T], in_=pl[:, 0:T]), [opl], [orl])
                P.op("dve", "tensor_tensor", dict(out=XN[:, h, 0:T], in0=po[:, 0:T], in1=rl[:, 0:T], op=ALU.mult),
                     [opo, orl], [self.o_XN])
        self.bi = 0

    def final_tile(self, T):
        c, P = self.cfg, self.P
        X = self.xv(T)
        rs, ors = self.emit_rstd(T)
        for kc in range(c.KC):
            P.op("dve", "scalar_tensor_tensor",
                 dict(out=X[:, kc, :], in0=X[:, kc, :], scalar=self.cst("g_fin", kc), in1=rs[:, 0:T],
                      op0=ALU.mult, op1=ALU.mult), [self.o_XIN, ors, self.o_CST], [self.o_XIN])

    def dump(self, name, tiles):
        c = self.cfg
        d = self.nc.dram_tensor("dbg_" + name, [c.NT, 128, c.KC * c.TMAX], F32, kind="ExternalOutput").ap()
        o = Obj("dbg_" + name, multi=True)
        for ti, (t0, T) in enumerate(tiles):
            self.load_x(self.hT, self.o_hT, ti, T)
            self.store_x(d, o, ti, T)
        self.dbg.append(o)

    def build(self, cols, stages=99):
        c, P, nc = self.cfg, self.P, self.nc
        self.cols = cols
        self.emit_casts()
        P.dma("act", self.CST[:, :], self.cst_d[:, :], [self.o_din], [self.o_CST], self.o_CST)
        P.op("dve", "memset", dict(ap=self.ONEB[:, :], constant=1.0), [], [self.o_ONEB])
        t1, t2 = c.tiles1, c.tiles2
        hT, oh = self.hT, self.o_hT
        nown = len(t2)
        self.load_x(self.h0T, self.o_h0, 0, t1[0][1])
        for ti, (t0, T) in enumerate(t1):
            def swap_a(ti=ti, T=T):
                self.store_x(hT, oh, ti, T)
                if ti + 1 < len(t1):
                    self.load_x(self.h0T, self.o_h0, ti + 1, t1[ti + 1][1])
            self.ffn_tile("a0", "g_a0", T)
            self.conv_in_tile(t0, T, after_norm=swap_a)
        if c.debug:
            self.dump("passA", t1)
        self.load_x(hT, oh, 0, t1[0][1])
        for ti, (t0, T) in enumerate(t1):
            def swap_b(ti=ti, T=T):
                if ti < nown or c.debug:
                    self.store_x(hT, oh, ti, T)
                if ti + 1 < len(t1):
                    self.load_x(hT, oh, ti + 1, t1[ti + 1][1])
            self.conv_out_tile(t0, T)
            self.ffn_tile("b0", "g_b0", T)
            self.ffn_tile("a1", "g_a1", T)
            self.kv_tile(t0, T, after_norm=swap_b)
        if c.debug:
            self.dump("passB", t1)
        for ti, (t0, T) in enumerate(t2):
            self.load_x(hT, oh, ti, T)
            self.attn_tile(t0, T)
            self.load_x(hT, oh, ti, T)
            self.resid_pairs("wo", c.KC, self.XN, [self.o_XN], T, 1.0)
            self.ffn_tile("b1", "g_b1", T)
            self.final_tile(T)
            self.store_x(self.outT, self.o_out, ti, T)
        P.wait_obj("act", [self.o_out] + self.dbg)
        engs = {"pe": "tensor", "act": "scalar", "dve": "vector", "pool": "gpsimd", "sp": "sync"}
        with nc.Block() as block:
            for k, attr in engs.items():
                def body(e, lst=P.ops[k]):
                    for f in lst:
                        f(e)
                getattr(block, attr)(body)
        return nc


def build_program(cfg, ncst_cols, cols, stages=99):
    import contextlib
    stack = contextlib.ExitStack()
    b = Builder(cfg, stack)
    b.alloc(ncst_cols)
    nc = b.build(cols, stages)
    return nc, b, stack


def make_in_maps(cfg, inp):
    ws = weight_streams(cfg, inp)
    cst, cols = small_consts(cfg, inp)
    x, meta = inp["x"], inp["meta_tokens"]
    maps = []
    for core in range(8):
        b, qi = divmod(core, 4)
        order = token_order(cfg, qi)
        seq = np.concatenate([meta, x[b]], axis=0)
        m = {"wf_" + k: v for k, v in ws.items()}
        m["h0T"] = block_tiles(cfg, seq[order].T, cfg.tiles1)
        m["rope_cs"] = rope_tables(cfg, order)
        m["conv_mask"] = conv_masks(cfg, order)
        m["consts"] = cst
        maps.append(m)
    return maps, cst.shape[1], cols


def run(cfg, inp, stages=99, trace=False):
    maps, ncst, cols = make_in_maps(cfg, inp)
    nc, b, stack = build_program(cfg, ncst, cols, stages)
    with stack:
        res = run_bass_kernel_spmd(nc, maps, core_ids=list(range(8)), trace=trace)
    out = np.empty((2, cfg.SEQ, cfg.D), np.float32)
    for core in range(8):
        bb, qi = divmod(core, 4)
        oT = unblock_tiles(cfg, res.results[core]["outT"], cfg.tiles2, cfg.NQ)
        out[bb, qi * cfg.NQ:(qi + 1) * cfg.NQ, :] = oT.T
    return out, res, b


def kernel(**inputs):
    inp = {k: np.asarray(v, dtype=np.float32) for k, v in inputs.items()}
    cfg = Cfg()
    out, _, _ = run(cfg, inp)
    return out
```
